# Optimizing a Trainium2 kernel written in Bass

```python
import math
import jax, jax.numpy as jnp
from jax import lax
import numpy as np

D_MODEL = 2048
BATCH = 1
SEQ = 16384
DEPTH = 1
DEC_BATCH = 4
DEC_SEQ = 2048
PAST_LEN = 128

N_META = 16
CHUNK = 64
DN_HEADS = 8
DN_HEAD_DIM = 128
DN_WIDTH = DN_HEADS * DN_HEAD_DIM
CONV_K = 5
FN_GROUPS = 4
FN_GROUP_DIM = 256
FN_WIDTH = FN_GROUPS * FN_GROUP_DIM
N_BRANCH = 2
DEEPNORM_ALPHA = (2 * DEPTH) ** 0.25
DEEPNORM_BETA = (8 * DEPTH) ** -0.25
LN_EPS = 1e-5
RMS_EPS = 1e-6
L2_EPS = 1e-6

COL_Q = 0
COL_K = COL_Q + DN_WIDTH
COL_V = COL_K + DN_WIDTH
COL_ZA = COL_V + DN_WIDTH
COL_B = COL_ZA + DN_WIDTH
COL_A = COL_B + 2 * DN_HEADS
COL_F = COL_A + 2 * DN_HEADS
COL_ZF = COL_F + FN_WIDTH
COL_G = COL_ZF + FN_WIDTH
D_IN = COL_G + N_BRANCH * D_MODEL

kernel_name = "hybrid_deltanet_fnet_encoder"


def layer_norm(x, g, b):
    xf = x.astype(jnp.float32)
    mu = jnp.mean(xf, axis=-1, keepdims=True)
    var = jnp.mean(jnp.square(xf - mu), axis=-1, keepdims=True)
    return ((xf - mu) * lax.rsqrt(var + LN_EPS) * g.astype(jnp.float32) + b.astype(jnp.float32)).astype(x.dtype)


def l2norm(x):
    return x * lax.rsqrt(jnp.sum(x * x, axis=-1, keepdims=True) + L2_EPS)


def centred_depthwise_conv(x, w):
    c = x.shape[-1]
    return lax.conv_general_dilated(
        x, w[:, None, :].astype(x.dtype), window_strides=(1,),
        padding=[(CONV_K // 2, CONV_K // 2)],
        dimension_numbers=("NWC", "WIO", "NWC"), feature_group_count=c)


def to_chunk_layout(t):
    pad = jnp.zeros((t.shape[0], CHUNK - N_META) + t.shape[2:], t.dtype)
    return jnp.concatenate([t[:, :N_META], pad, t[:, N_META:]], axis=1)


def from_chunk_layout(t):
    return jnp.concatenate([t[:, :N_META], t[:, CHUNK:]], axis=1)


def chunk_gated_delta_rule(q, k, v, g, beta):
    B, Tp, H, dk = q.shape
    dv = v.shape[-1]
    n = Tp // CHUNK
    qc = jnp.transpose(q.reshape(B, n, CHUNK, H, dk), (1, 0, 3, 2, 4))
    kc = jnp.transpose(k.reshape(B, n, CHUNK, H, dk), (1, 0, 3, 2, 4))
    vc = jnp.transpose(v.reshape(B, n, CHUNK, H, dv), (1, 0, 3, 2, 4))
    gch = jnp.transpose(g.reshape(B, n, CHUNK, H), (1, 0, 3, 2))
    bch = jnp.transpose(beta.reshape(B, n, CHUNK, H), (1, 0, 3, 2))

    gc = jnp.cumsum(gch, axis=-1)
    idx = jnp.arange(CHUNK)
    incl = idx[:, None] >= idx[None, :]
    strict = idx[:, None] > idx[None, :]
    decay = jnp.exp(jnp.where(incl, gc[..., :, None] - gc[..., None, :], -jnp.inf))

    kb = kc * bch[..., None]
    lower = jnp.where(strict, jnp.einsum('nbhcd,nbhsd->nbhcs', kb, kc) * decay, 0.0)
    a_mat = lower + jnp.eye(CHUNK, dtype=jnp.float32)
    u = lax.linalg.triangular_solve(a_mat, vc * bch[..., None], left_side=True, lower=True, unit_diagonal=True)
    w = lax.linalg.triangular_solve(a_mat, kb * jnp.exp(gc)[..., None], left_side=True, lower=True, unit_diagonal=True)

    qk = jnp.einsum('nbhcd,nbhsd->nbhcs', qc, kc) * decay
    q_dec = qc * jnp.exp(gc)[..., None]
    k_dec = kc * jnp.exp(gc[..., -1:] - gc)[..., None]
    g_last = jnp.exp(gc[..., -1])

    def step(state, xs):
        qk_n, qd_n, kd_n, u_n, w_n, gl_n = xs
        v_new = u_n - jnp.einsum('bhcd,bhde->bhce', w_n, state)
        o = jnp.einsum('bhcd,bhde->bhce', qd_n, state) + jnp.einsum('bhcs,bhse->bhce', qk_n, v_new)
        state = state * gl_n[..., None, None] + jnp.einsum('bhcd,bhce->bhde', kd_n, v_new)
        return state, o

    s0 = jnp.zeros((B, H, dk, dv), jnp.float32)
    _, o = lax.scan(step, s0, (qk, q_dec, k_dec, u, w, g_last))
    return jnp.transpose(o, (1, 0, 3, 2, 4)).reshape(B, Tp, H, dv)


def encoder_layer(h, w_in, b_gate, conv_w, a_log, dt_bias, dn_norm_w, w_proj_a, w_proj_f, w_out, ln_g, ln_b):
    B, T, _ = h.shape
    f32 = jnp.float32
    proj = jnp.einsum('btd,de->bte', h, w_in)

    qkv = jax.nn.silu(centred_depthwise_conv(proj[..., COL_Q:COL_ZA], conv_w)).astype(f32)
    q = l2norm(qkv[..., :DN_WIDTH].reshape(B, T, DN_HEADS, DN_HEAD_DIM)) * (DN_HEAD_DIM ** -0.5)
    k = l2norm(qkv[..., DN_WIDTH:2 * DN_WIDTH].reshape(B, T, DN_HEADS, DN_HEAD_DIM))
    v = qkv[..., 2 * DN_WIDTH:].reshape(B, T, DN_HEADS, DN_HEAD_DIM)
    beta = jax.nn.sigmoid(proj[..., COL_B:COL_A].astype(f32)).reshape(B, T, 2, DN_HEADS)
    g = -jnp.exp(a_log.astype(f32)) * jax.nn.softplus(
        proj[..., COL_A:COL_F].astype(f32).reshape(B, T, 2, DN_HEADS) + dt_bias.astype(f32))

    qp, kp, vp, bp, gp = (to_chunk_layout(t) for t in (q, k, v, beta, g))
    o_fwd = chunk_gated_delta_rule(qp, kp, vp, gp[:, :, 0], bp[:, :, 0])
    rev = lambda t: jnp.flip(t, axis=1)
    o_bwd = rev(chunk_gated_delta_rule(rev(qp), rev(kp), rev(vp), rev(gp[:, :, 1]), rev(bp[:, :, 1])))
    o = from_chunk_layout(o_fwd + o_bwd)

    o = o * lax.rsqrt(jnp.mean(o * o, axis=-1, keepdims=True) + RMS_EPS) * dn_norm_w.astype(f32)
    z_a = proj[..., COL_ZA:COL_B].astype(f32).reshape(B, T, DN_HEADS, DN_HEAD_DIM)
    y_a = (o * jax.nn.silu(z_a)).reshape(B, T, DN_WIDTH).astype(h.dtype)

    f = proj[..., COL_F:COL_ZF].astype(f32).reshape(B, T, FN_GROUPS, FN_GROUP_DIM)
    f = jnp.real(jnp.fft.fft2(f, axes=(1, 3), norm='ortho')).reshape(B, T, FN_WIDTH)
    y_f = (f * jax.nn.silu(proj[..., COL_ZF:COL_G].astype(f32))).astype(h.dtype)

    gates = jax.nn.sigmoid((proj[..., COL_G:] + b_gate).astype(f32))
    m = (gates[..., :D_MODEL] * jnp.einsum('bte,ed->btd', y_a, w_proj_a).astype(f32)
         + gates[..., D_MODEL:] * jnp.einsum('bte,ed->btd', y_f, w_proj_f).astype(f32))
    out = jnp.einsum('btd,de->bte', m.astype(h.dtype), w_out)

    return layer_norm(DEEPNORM_ALPHA * h + out, ln_g, ln_b)


def encode(x, meta_tokens, ln_in_g, ln_in_b, w_in, b_gate, conv_w, a_log, dt_bias, dn_norm_w,
           w_proj_a, w_proj_f, w_out, ln_g, ln_b):
    B = x.shape[0]
    meta = jnp.broadcast_to(meta_tokens[None].astype(x.dtype), (B, N_META, D_MODEL))
    h = layer_norm(jnp.concatenate([meta, x], axis=1), ln_in_g, ln_in_b)
    for i in range(DEPTH):
        h = encoder_layer(h, w_in[i], b_gate[i], conv_w[i], a_log[i], dt_bias[i], dn_norm_w[i],
                          w_proj_a[i], w_proj_f[i], w_out[i], ln_g[i], ln_b[i])
    return h[:, N_META:]


def setup_inputs(seed: int = 0) -> dict:
    key = jax.random.key(seed)
    ks = jax.random.split(key, 18)
    nrm = lambda k, s, sc: jax.random.normal(k, s, jnp.float32) * sc
    dt = jnp.exp(jax.random.uniform(ks[8], (DEPTH, 2, DN_HEADS), jnp.float32,
                                    math.log(1e-3), math.log(1e-1)))
    return {
        "x_prompt": nrm(ks[0], (BATCH, SEQ, D_MODEL), 1.0),
        "x_sample": nrm(ks[1], (DEC_BATCH, DEC_SEQ, D_MODEL), 1.0),
        "meta_tokens": nrm(ks[2], (N_META, D_MODEL), 1.0),
        "ln_in_g": 1.0 + nrm(ks[3], (D_MODEL,), 0.02),
        "ln_in_b": nrm(ks[4], (D_MODEL,), 0.02),
        "w_in": nrm(ks[5], (DEPTH, D_MODEL, D_IN), D_MODEL ** -0.5),
        "b_gate": nrm(ks[6], (DEPTH, N_BRANCH * D_MODEL), 0.02),
        "conv_w": nrm(ks[7], (DEPTH, CONV_K, 3 * DN_WIDTH), CONV_K ** -0.5),
        "a_log": jnp.log(jax.random.uniform(ks[9], (DEPTH, 2, DN_HEADS), jnp.float32, 1.0, 16.0)),
        "dt_bias": dt + jnp.log(-jnp.expm1(-dt)),
        "dn_norm_w": 1.0 + nrm(ks[10], (DEPTH, DN_HEAD_DIM), 0.02),
        "w_proj_a": nrm(ks[11], (DEPTH, DN_WIDTH, D_MODEL), DN_WIDTH ** -0.5 * DEEPNORM_BETA),
        "w_proj_f": nrm(ks[12], (DEPTH, FN_WIDTH, D_MODEL), FN_WIDTH ** -0.5 * DEEPNORM_BETA),
        "w_out": nrm(ks[13], (DEPTH, D_MODEL, D_MODEL), D_MODEL ** -0.5 * DEEPNORM_BETA),
        "ln_g": 1.0 + nrm(ks[14], (DEPTH, D_MODEL), 0.02),
        "ln_b": nrm(ks[15], (DEPTH, D_MODEL), 0.02),
    }


def reference(x_prompt, x_sample, meta_tokens, ln_in_g, ln_in_b, w_in, b_gate, conv_w, a_log, dt_bias,
              dn_norm_w, w_proj_a, w_proj_f, w_out, ln_g, ln_b):
    y_prompt = encode(x_prompt, meta_tokens, ln_in_g, ln_in_b, w_in, b_gate, conv_w, a_log, dt_bias,
                      dn_norm_w, w_proj_a, w_proj_f, w_out, ln_g, ln_b)
    y_sample = encode(x_sample, meta_tokens, ln_in_g, ln_in_b, w_in, b_gate, conv_w, a_log, dt_bias,
                      dn_norm_w, w_proj_a, w_proj_f, w_out, ln_g, ln_b)
    return (y_prompt, y_sample)
```

```python
import math
from contextlib import ExitStack
import numpy as np
import ml_dtypes
import concourse.bass as bass
import concourse.mybir as mybir
from concourse.bass_utils import run_bass_kernel_spmd

F32 = mybir.dt.float32
BF16 = mybir.dt.bfloat16
I32 = mybir.dt.int32
AF = mybir.ActivationFunctionType
ALU = mybir.AluOpType

D_MODEL = 2048
KD = 16
N_META = 16
DN_HEADS = 8
COL_Q, COL_K, COL_V, COL_ZA = 0, 1024, 2048, 3072
COL_B = 4096
COL_A = COL_B + 16
COL_F = COL_A + 16
COL_ZF = COL_F + 1024
COL_G = COL_ZF + 1024
ALPHA = 2.0 ** 0.25
LN_EPS = 1e-5
RMS_EPS = 1e-6
L2_EPS = 1e-6
NCORE = 8
NEG = -30000.0


class StopBuild(Exception):
    pass


class Sched:
    def __init__(self, nc, n_dma_sems=24):
        self.nc = nc
        self.eng = {"pe": nc.tensor, "dve": nc.vector, "act": nc.scalar, "pool": nc.gpsimd, "sp": nc.sync}
        self.sem = {k: nc.alloc_semaphore("sem_" + k) for k in ["pe", "dve", "act", "pool"]}
        self.cnt = {k: 0 for k in self.sem}
        self.dsem = [nc.alloc_semaphore("dsem%d" % i) for i in range(n_dma_sems)]
        self.dcnt = [0] * n_dma_sems
        self.dpool = {"sp": list(range(0, n_dma_sems - 8)), "pool": list(range(n_dma_sems - 8, n_dma_sems))}
        self.dpool["act"] = self.dpool["sp"]
        self.dnext = {"sp": 0, "pool": 0, "act": 0}
        self.defq = "act"
        self.waited = {k: {} for k in self.eng}
        self.bufs = {}
        self.nins = 0

    def _need(self, e, tok):
        if tok is None:
            return
        key, val = tok[:2], tok[2]
        if key[0] == "c" and key[1] == e and e == "pe":
            return
        if self.waited[e].get(key, 0) >= val:
            return
        self.waited[e][key] = val
        sem = self.sem[key[1]] if key[0] == "c" else self.dsem[key[1]]
        self.eng[e].wait_ge(sem, val)

    def _deps(self, e, reads, writes):
        st = self.bufs
        for r in reads:
            b = st.setdefault(r, {"w": [], "r": []})
            for t in b["w"]:
                self._need(e, t)
            if r.startswith("ps"):
                for t in b["r"]:
                    if not (t[0] == "c" and t[1] == e):
                        self._need(e, t)
        for w in writes:
            b = st.setdefault(w, {"w": [], "r": []})
            for t in b["w"]:
                self._need(e, t)
            for t in b["r"]:
                self._need(e, t)

    def _mark(self, tok, reads, writes, multi):
        st = self.bufs
        for r in reads:
            st[r]["r"].append(tok)
            if len(st[r]["r"]) > 64:
                st[r]["r"] = self._compress(st[r]["r"])
        for w in writes:
            if multi:
                st[w]["w"].append(tok)
                if len(st[w]["w"]) > 64:
                    st[w]["w"] = self._compress(st[w]["w"])
            else:
                st[w]["w"] = [tok]
                st[w]["r"] = []

    @staticmethod
    def _compress(toks):
        best = {}
        for t in toks:
            k = t[:2]
            if k not in best or best[k][2] < t[2]:
                best[k] = t
        return list(best.values())

    def op(self, e, fn, reads=(), writes=(), inc=True):
        self._deps(e, reads, writes)
        ins = fn()
        self.nins += 1
        tok = ("c", e, self.cnt[e] + 1)
        if inc:
            ins.then_inc(self.sem[e], 1)
            self.cnt[e] += 1
        self._mark(tok, reads, writes, False)
        return ins

    def dma(self, out, in_, reads=(), writes=(), q="sp", multi=False, fn=None):
        if fn is None and out.dtype != in_.dtype:
            q = "pool"
        if q == "sp":
            q = self.defq
        lst = self.dpool[q]
        i = lst[self.dnext[q] % len(lst)]
        self.dnext[q] += 1
        if self.dcnt[i] > 0:
            self._need(q, ("d", i, 16 * self.dcnt[i]))
        if multi:
            self._deps(q, reads, ())
            for w in writes:
                b = self.bufs.setdefault(w, {"w": [], "r": []})
                for t in b["r"]:
                    self._need(q, t)
        else:
            self._deps(q, reads, writes)
        self.dcnt[i] += 1
        tok = ("d", i, 16 * self.dcnt[i])
        if fn is None:
            ins = self.eng[q].dma_start(out=out, in_=in_)
        else:
            ins = fn()
        ins.then_inc(self.dsem[i], 16)
        self.nins += 1
        self._mark(tok, reads, writes, multi)

    def barrier(self):
        for e in self.eng:
            for i, c in enumerate(self.dcnt):
                if c:
                    self._need(e, ("d", i, 16 * c))
            for k, c in self.cnt.items():
                if c:
                    if k == e and e == "pe":
                        continue
                    self._need(e, ("c", k, c))

    def finish(self, q="sp"):
        for i, c in enumerate(self.dcnt):
            if c:
                self._need(q, ("d", i, 16 * c))
        for k, c in self.cnt.items():
            if c:
                self._need(q, ("c", k, c))


def seq_layout(seq_lens):
    lay = []
    base = 0
    for T in seq_lens:
        Tp = -(-T // 128) * 128
        lay.append((base, T, Tp))
        base += Tp
    TG = -(-base // 1024) * 1024
    return lay, TG


def fft_factors(T):
    best = None
    for n1 in range(1, 129):
        if T % n1 == 0:
            n2 = T // n1
            if n2 <= 256:
                cost = (-(-n2 // 128)) * 1000 + abs(n1 - n2)
                if best is None or cost < best[0]:
                    best = (cost, n1, n2)
    assert best is not None, T
    return best[1], best[2]


def build_program(seq_lens, debug=False, stop=None, fac_override=None):
    lay, TG = seq_layout(seq_lens)
    NB = TG // 512
    TC = TG // NCORE
    NTC = TC // 128
    ftypes = sorted(set(T for (_, T, _) in lay))
    fac = {T: fft_factors(T) for T in ftypes}
    if fac_override:
        fac.update(fac_override)

    nc = bass.Bass("TRN2", target_bir_lowering=False)
    dt_in = lambda name, shape, dt=F32: nc.dram_tensor(name, list(shape), dt, kind="ExternalInput").ap()
    scr = lambda name, shape, dt: nc.dram_tensor(name, list(shape), dt, kind=("ExternalOutput" if debug else "Internal")).ap()

    xall = dt_in("xall", [TG, D_MODEL])
    xc = dt_in("xc", [TC, D_MODEL])
    wc = dt_in("wc", [D_MODEL, 1024])
    lng_pk = dt_in("lng_pk", [128, KD])
    lnb_pk = dt_in("lnb_pk", [128, KD])
    cw = dt_in("cw", [128, 15])
    alog = dt_in("alog", [128, 2])
    dtb = dt_in("dtb", [128, 2])
    dnw = dt_in("dnw", [128, 128])
    ident_in = dt_in("ident", [128, 128])
    maskA_in = dt_in("maskA", [128, 2, 144])
    maskB_in = dt_in("maskB", [128, 2, 144])
    negL_in = dt_in("negL", [128, 2, 128])
    negQ_in = dt_in("negQ", [128, 2, 128])
    dch_in = dt_in("dch", [128, 2, 256])
    tabs = {}
    for T in ftypes:
        n1, n2 = fac[T]
        tabs[T] = dict(
            c1=dt_in("c1_%d" % T, [n1, 3, n1]),
            g=dt_in("g_%d" % T, [n2, 2, n1, n2]),
        )
    wg_in = dt_in("wg", [D_MODEL, 4096])
    wpa_in = dt_in("wpa", [1024, D_MODEL])
    wpf_in = dt_in("wpf", [1024, D_MODEL])
    wo_in = dt_in("wo", [D_MODEL, D_MODEL])
    bc_in = dt_in("bcast", [128, 4, D_MODEL])
    bg_in = dt_in("bgate", [1, 4096])
    gidx_in = dt_in("gidx", [128, NTC, 16], I32)
    out = nc.dram_tensor("out", [TC, D_MODEL], F32, kind="ExternalOutput").ap()

    QKV = scr("s_qkv", [3, 128, TG], F32)
    ZAS = scr("s_zas", [128, TG], BF16)
    ZFS = scr("s_zfs", [128, TG], BF16)
    BA = scr("s_ba", [4, TG], F32)
    Z = scr("s_z", [TG, 256], BF16)
    YB = scr("s_yb", [max(fac[T][0] * fac[T][1] for T in ftypes) * 256], BF16)
    SEND = nc.dram_tensor("s_send", [NCORE * 256, TC], BF16, kind="Internal").ap()
    RECV = nc.dram_tensor("s_recv", [NCORE * NCORE * 256, TC], BF16, kind="Internal").ap()
    YDBG = scr("s_ydbg", [256, TG], BF16) if debug else None
    WG16 = nc.dram_tensor("s_wg16", [D_MODEL, 4096], BF16, kind="Internal").ap()
    WPA16 = nc.dram_tensor("s_wpa16", [1024, D_MODEL], BF16, kind="Internal").ap()
    WPF16 = nc.dram_tensor("s_wpf16", [1024, D_MODEL], BF16, kind="Internal").ap()
    WO16 = nc.dram_tensor("s_wo16", [D_MODEL, D_MODEL], BF16, kind="Internal").ap()

    S = Sched(nc)
    es_glob = ExitStack()
    es_cur = [es_glob]

    def sb(name, shape, dt=F32):
        return es_cur[0].enter_context(nc.sbuf_tensor("sb_" + name, list(shape), dt))

    psF = [nc.alloc_psum_tensor("psF%d" % i, [128, 512], F32) for i in range(6)]
    psB = [nc.alloc_psum_tensor("psB%d" % i, [128, 1024], BF16) for i in range(2)]
    bank_rr = [0]

    def fbank():
        i = bank_rr[0]
        bank_rr[0] = (i + 1) % 6
        return psF[i], "psF%d" % i

    bb_rr = [0]

    def bbank():
        i = bb_rr[0]
        bb_rr[0] = (i + 1) % 2
        return psB[i], "psB%d" % i

    ident_b = sb("ident_b", [128, 128], BF16)
    ident_f = sb("ident_f", [128, 128], F32)
    maskA = sb("maskA", [128, 2, 144], F32)
    maskB = sb("maskB", [128, 2, 144], F32)
    negL = sb("negL", [128, 2, 128], BF16)
    maskA_b = sb("maskA_b", [128, 2, 144], BF16)
    maskB_b = sb("maskB_b", [128, 2, 144], BF16)
    negQ = sb("negQ", [128, 2, 128], BF16)
    dch = sb("dch", [128, 2, 256], BF16)
    ones_b = sb("ones_b", [128, 128], BF16)
    ones_f = sb("ones_f", [128, 128], F32)
    cw_sb = sb("cw_sb", [128, 15], F32)
    alog_sb = sb("alog_sb", [128, 2], F32)
    negA = sb("negA", [128, 2], F32)
    dtb_sb = sb("dtb_sb", [128, 2], F32)
    dnw_sb = sb("dnw_sb", [128, 128], F32)
    S.dma(ident_b[:], ident_in, writes=["ident_b"])
    S.dma(ident_f[:], ident_in, writes=["ident_f"])
    S.dma(maskA[:], maskA_in, writes=["maskA"])
    S.dma(maskB[:], maskB_in, writes=["maskB"])
    S.dma(negL[:], negL_in, writes=["negL"])
    S.dma(maskA_b[:], maskA_in, writes=["maskA_b"])
    S.dma(maskB_b[:], maskB_in, writes=["maskB_b"])
    S.dma(negQ[:], negQ_in, writes=["negQ"])
    S.dma(dch[:], dch_in, writes=["dch"])
    S.dma(cw_sb[:], cw, writes=["cw"])
    S.dma(alog_sb[:], alog, writes=["alog"])
    S.dma(dtb_sb[:], dtb, writes=["dtb"])
    S.dma(dnw_sb[:], dnw, writes=["dnw"])
    S.op("pool", lambda: nc.gpsimd.memset(ones_b[:], 1.0), writes=["ones_b"])
    S.op("pool", lambda: nc.gpsimd.memset(ones_f[:], 1.0), writes=["ones_f"])
    for (dst_, src_, nm_, rows_) in ((WG16, wg_in, "wg16", D_MODEL), (WPA16, wpa_in, "wpa16", 1024), (WPF16, wpf_in, "wpf16", 1024), (WO16, wo_in, "wo16", D_MODEL)):
        for r0 in range(0, rows_, 512):
            S.dma(dst_[r0:r0 + 512, :], src_[r0:r0 + 512, :], writes=[nm_], q="pool", multi=True)
    zero_t = sb("zero_t", [128, TC], BF16)
    S.op("pool", lambda: nc.gpsimd.memset(zero_t[:], 0.0), writes=["zero_t"])
    for r0 in range(0, NCORE * 256, 128):
        S.dma(SEND[r0:r0 + 128, :], zero_t[:], reads=["zero_t"], writes=["send0"], multi=True)
    S.op("act", lambda: nc.scalar.activation(out=negA[:], in_=alog_sb[:], func=AF.Exp), reads=["alog"], writes=["negA"])
    S.op("dve", lambda: nc.vector.tensor_scalar(out=negA[:], in0=negA[:], scalar1=-1.0, scalar2=None, op0=ALU.mult), reads=["negA"], writes=["negA"])

    esA = ExitStack()
    es_cur[0] = esA
    wb = sb("wb", [128, KD, 1024], BF16)
    wtmp = sb("wtmp", [128, KD, 256], F32)
    g_pk = sb("g_pk", [128, KD], F32)
    b_pk = sb("b_pk", [128, 2, KD], F32)
    biasA = sb("biasA", [128, 8], F32)
    S.dma(g_pk[:], lng_pk, writes=["g_pk"])
    S.op("pool", lambda: nc.gpsimd.memset(b_pk[:], 0.0), writes=["b_pk"])
    S.dma(b_pk[:, 0, :], lnb_pk, writes=["b_pk"])
    wc_v = wc.rearrange("(k p) n -> p k n", p=128)
    for q4 in range(4):
        S.dma(wtmp[:], wc_v[:, :, q4 * 256:(q4 + 1) * 256], writes=["wtmp"])
        for ct in (2 * q4, 2 * q4 + 1):
            ps, pn = fbank()
            for k in range(KD):
                S.op("pe", lambda: nc.tensor.matmul(ps[:, 0:2], lhsT=wtmp[:, k, (ct % 2) * 128:(ct % 2) * 128 + 128], rhs=b_pk[:, :, k],
                                                     start=(k == 0), stop=(k == KD - 1)),
                     reads=["wtmp", "b_pk"], writes=[pn], inc=(k == KD - 1))
            S.op("dve", lambda: nc.vector.tensor_copy(out=biasA[:, ct:ct + 1], in_=ps[:, 0:1]), reads=[pn], writes=["biasA"])
        for k in range(KD):
            e = "dve" if k % 2 == 0 else "pool"
            en = nc.vector if k % 2 == 0 else nc.gpsimd
            S.op(e, lambda: en.tensor_scalar(out=wb[:, k, q4 * 256:(q4 + 1) * 256], in0=wtmp[:, k, :], scalar1=g_pk[:, k:k + 1], scalar2=None, op0=ALU.mult),
                 reads=["wtmp", "g_pk"], writes=["wb"])

    NXB = 3
    xt = [sb("xt%d" % i, [128, D_MODEL], F32) for i in range(NXB)]
    hb = [sb("hb%d" % i, [128, D_MODEL], BF16) for i in range(2)]
    hT = [sb("hT%d" % i, [128, KD, 512], BF16) for i in range(2)]
    st6 = sb("st6", [128, 4, 4, 6], F32)
    mv = sb("mv", [128, 4, 2], F32)
    rstd = sb("rstd", [128, 4], F32)
    nmr = sb("nmr", [128, 4], F32)
    stg32 = [sb("stg32_%d" % i, [128, 512], F32) for i in range(3)]
    stg16 = [sb("stg16_%d" % i, [128, 512], BF16) for i in range(2)]
    fT = [sb("fT%d" % i, [128, 2, 512], BF16) for i in range(2)]
    zst = [sb("zst%d" % i, [128, 4, 256], BF16) for i in range(2)]
    eps_t = sb("eps_t", [128, 1], F32)
    S.op("pool", lambda: nc.gpsimd.memset(eps_t[:], LN_EPS), writes=["eps_t"])
    xcnt = 0
    s32 = 0
    s16 = 0
    for blk in range(NB):
        hTb = hT[blk % 2]
        hTn = "hT%d" % (blk % 2)
        xs = []
        for sub in range(4):
            xi = xcnt % NXB
            xcnt += 1
            xs.append(xi)
            r0 = blk * 512 + sub * 128
            S.dma(xt[xi][:], xall[r0:r0 + 128, :], writes=["xt%d" % xi])
            for c in range(4):
                S.op("dve", lambda: nc.vector.bn_stats(out=st6[:, sub, c, :], in_=xt[xi][:, c * 512:(c + 1) * 512]),
                     reads=["xt%d" % xi], writes=["st6_%d" % sub])
            S.op("dve", lambda: nc.vector.bn_aggr(out=mv[:, sub, :], in_=st6[:, sub, :, :]), reads=["st6_%d" % sub], writes=["mv%d" % sub])
            S.op("act", lambda: nc.scalar.activation(out=rstd[:, sub:sub + 1], in_=mv[:, sub, 1:2], func=AF.Sqrt, bias=eps_t[:], scale=1.0),
                 reads=["mv%d" % sub, "eps_t"], writes=["rstd%d" % sub])
            S.op("dve", lambda: nc.vector.reciprocal(out=rstd[:, sub:sub + 1], in_=rstd[:, sub:sub + 1]), reads=["rstd%d" % sub], writes=["rstd%d" % sub])
            S.op("dve", lambda: nc.vector.scalar_tensor_tensor(out=nmr[:, sub:sub + 1], in0=mv[:, sub, 0:1], scalar=-1.0, in1=rstd[:, sub:sub + 1],
                                                                op0=ALU.mult, op1=ALU.mult),
                 reads=["mv%d" % sub, "rstd%d" % sub], writes=["nmr%d" % sub])
            hi = (blk * 4 + sub) % 2
            S.op("act", lambda: nc.scalar.activation(out=hb[hi][:], in_=xt[xi][:], func=AF.Identity, bias=nmr[:, sub:sub + 1], scale=rstd[:, sub:sub + 1]),
                 reads=["xt%d" % xi, "nmr%d" % sub, "rstd%d" % sub], writes=["hb%d" % hi])
            for kg in range(4):
                pb, pbn = bbank()
                for kk in range(4):
                    k = kg * 4 + kk
                    S.op("pe", lambda: nc.tensor.transpose(out=pb[:, kk * 128:(kk + 1) * 128], in_=hb[hi][:, k * 128:(k + 1) * 128], identity=ident_b[:]),
                         reads=["hb%d" % hi, "ident_b"], writes=[pbn], inc=(kk == 3))
                e = "dve" if kg % 2 == 0 else "act"
                if e == "dve":
                    S.op("dve", lambda: nc.vector.tensor_copy(out=hTb[:, kg * 4:kg * 4 + 4, sub * 128:(sub + 1) * 128],
                                                               in_=pb[:, 0:512].rearrange("p (a b) -> p a b", a=4)),
                         reads=[pbn], writes=[hTn])
                else:
                    S.op("act", lambda: nc.scalar.copy(out=hTb[:, kg * 4:kg * 4 + 4, sub * 128:(sub + 1) * 128],
                                                        in_=pb[:, 0:512].rearrange("p (a b) -> p a b", a=4)),
                         reads=[pbn], writes=[hTn])
        c0 = blk * 512
        for ct in range(8):
            ps, pn = fbank()
            for k in range(KD):
                S.op("pe", lambda: nc.tensor.matmul(ps[:], lhsT=wb[:, k, ct * 128:(ct + 1) * 128], rhs=hTb[:, k, :], start=(k == 0), stop=(k == KD - 1)),
                     reads=["wb", hTn], writes=[pn], inc=(k == KD - 1))
            if ct < 3 or ct == 7:
                o = stg32[s32 % 3]
                on = "stg32_%d" % (s32 % 3)
                s32 += 1
                np_ = 128 if ct < 3 else 4
                S.op("act", lambda: nc.scalar.activation(out=o[0:np_, :], in_=ps[0:np_, :], func=AF.Identity, bias=biasA[0:np_, ct:ct + 1], scale=1.0),
                     reads=[pn, "biasA"], writes=[on])
                if ct < 3:
                    S.dma(QKV[ct, :, c0:c0 + 512], o[:], reads=[on], writes=["qkv_b%d" % blk], multi=True)
                else:
                    S.dma(BA[:, c0:c0 + 512], o[0:4, :], reads=[on], writes=["ba_b%d" % blk], multi=True)
            elif ct in (3, 6):
                o = stg16[s16 % 2]
                on = "stg16_%d" % (s16 % 2)
                s16 += 1
                S.op("act", lambda: nc.scalar.activation(out=o[:], in_=ps[:], func=AF.Silu, bias=biasA[:, ct:ct + 1], scale=1.0),
                     reads=[pn, "biasA"], writes=[on])
                dst = ZAS if ct == 3 else ZFS
                S.dma(dst[:, c0:c0 + 512], o[:], reads=[on], writes=[("zas_b%d" if ct == 3 else "zfs_b%d") % blk], multi=True)
            else:
                fb = fT[blk % 2]
                fn_ = "fT%d" % (blk % 2)
                S.op("dve", lambda: nc.vector.tensor_scalar(out=fb[:, ct - 4, :], in0=ps[:], scalar1=biasA[:, ct:ct + 1], scalar2=None, op0=ALU.add),
                     reads=[pn, "biasA"], writes=[fn_ + "_%d" % (ct - 4)])
        fb = fT[blk % 2]
        fn_ = "fT%d" % (blk % 2)
        zb = zst[blk % 2]
        zn = "zst%d" % (blk % 2)
        for half in range(2):
            ps, pn = fbank()
            for s2 in range(2):
                sub = half * 2 + s2
                for kk in range(2):
                    S.op("pe", lambda: nc.tensor.matmul(ps[:, s2 * 256:(s2 + 1) * 256], lhsT=fb[:, kk, sub * 128:(sub + 1) * 128], rhs=dch[:, kk, :],
                                                         start=(kk == 0), stop=(kk == 1)),
                         reads=[fn_ + "_0", fn_ + "_1", "dch"], writes=[pn], inc=(kk == 1 and s2 == 1))
            S.op("dve", lambda: nc.vector.tensor_copy(out=zb[:, half * 2:half * 2 + 2, :], in_=ps[:].rearrange("p (a b) -> p a b", a=2)),
                 reads=[pn], writes=[zn])
        S.dma(Z[c0:c0 + 512, :].rearrange("(a p) n -> p a n", p=128), zb[:], reads=[zn], writes=["z_b%d" % blk], multi=True)

    if stop == "A":
        S.finish("sp")
        return nc, dict(lay=lay, TG=TG, TC=TC, NTC=NTC, fac=fac, nins=S.nins)
    S.barrier()
    esA.close()
    esB = ExitStack()
    es_cur[0] = esB
    win = [sb("win%d" % i, [128, 3, 132], F32) for i in range(2)]
    cacc = [sb("cacc%d" % i, [128, 3, 128], F32) for i in range(2)]
    qkv_s = sb("qkv_s", [128, 3, 128], F32)
    sq = sb("sq", [128, 2, 128], BF16)
    rs = sb("rs", [128, 2, 128], F32)
    l2eps = sb("l2eps", [128, 1], F32)
    S.op("pool", lambda: nc.gpsimd.memset(l2eps[:], L2_EPS), writes=["l2eps"])
    rmseps = sb("rmseps", [128, 1], F32)
    S.op("pool", lambda: nc.gpsimd.memset(rmseps[:], RMS_EPS), writes=["rmseps"])
    ba_f = sb("ba_f", [4, 128], F32)
    NS = 4
    qk_t = [sb("qk_t%d" % i, [128, 2, 128], BF16) for i in range(NS)]
    kv_t = [sb("kv_t%d" % i, [128, 2, 128], F32) for i in range(NS)]
    sm_t = [sb("sm_t%d" % i, [128, 8], F32) for i in range(NS)]
    qT = [qk_t[i][:, 0, :] for i in range(NS)]
    kT = [qk_t[i][:, 1, :] for i in range(NS)]
    vT = [sb("vT%d" % i, [128, 128], BF16) for i in range(NS)]
    ktok = [kv_t[i][:, 0, :] for i in range(NS)]
    vtok = [kv_t[i][:, 1, :] for i in range(NS)]
    batok = [sb("batok%d" % i, [128, 4], F32) for i in range(NS)]
    beta = [sm_t[i][:, 0:2] for i in range(NS)]
    nbeta = [sm_t[i][:, 2:4] for i in range(NS)]
    gtok = [sb("gtok%d" % i, [128, 2], F32) for i in range(NS)]
    ghb = [sb("ghb%d" % i, [128, 2], BF16) for i in range(NS)]
    ghl = [sm_t[i][:, 4:8].rearrange("p (a b) -> p a b", a=2) for i in range(NS)]
    NCHG = TG // 128
    PQK = nc.dram_tensor("s_pqk", [NCHG, 128, 256], BF16, kind="Internal").ap()
    PKV = nc.dram_tensor("s_pkv", [NCHG, 128, 256], F32, kind="Internal").ap()
    PSM = nc.dram_tensor("s_psm", [NCHG, 128, 8], F32, kind="Internal").ap()

    def spill_chunk(g, slot):
        S.dma(PQK[g], qk_t[slot][:].rearrange("p a t -> p (a t)"), reads=["qT%d" % slot, "kT%d" % slot], writes=["pc_%d" % g])
        S.dma(PKV[g], kv_t[slot][:].rearrange("p a t -> p (a t)"), reads=["ktok%d" % slot, "vtok%d" % slot], writes=["pc_%d" % g], multi=True)
        S.dma(PSM[g], sm_t[slot][:], reads=["beta%d" % slot, "nbeta%d" % slot, "ghl%d" % slot], writes=["pc_%d" % g], multi=True)

    def reload_chunk(g, slot):
        S.dma(qk_t[slot][:].rearrange("p a t -> p (a t)"), PQK[g], reads=["pc_%d" % g], writes=["qT%d" % slot, "kT%d" % slot])
        S.dma(kv_t[slot][:].rearrange("p a t -> p (a t)"), PKV[g], reads=["pc_%d" % g], writes=["ktok%d" % slot, "vtok%d" % slot])
        S.dma(sm_t[slot][:], PSM[g], reads=["pc_%d" % g], writes=["beta%d" % slot, "nbeta%d" % slot, "ghl%d" % slot])

    def prep_chunk(si, c, slot):
        base, T, Tp = lay[si]
        t0 = base + 128 * c
        nv = min(128, T - 128 * c)
        lo = 2 if c > 0 else 0
        hi_ = min(2, T - 128 * c - nv)
        w = win[c % 2]
        wn = "win%d" % (c % 2)
        blks = sorted(set([(t0 - lo) // 512, (t0 + nv + hi_ - 1) // 512]))
        if nv < 128 or lo < 2 or hi_ < 2:
            S.op("pool", lambda: nc.gpsimd.memset(w[:], 0.0), writes=[wn])
        S.dma(w[:, :, 2 - lo:2 + nv + hi_], QKV[:, :, t0 - lo:t0 + nv + hi_].rearrange("a p t -> p a t"),
              reads=["qkv_b%d" % b for b in blks], writes=[wn])
        ca = cacc[c % 2]
        can = "cacc%d" % (c % 2)
        for a in range(3):
            for j in range(5):
                if j == 0:
                    S.op("dve", lambda: nc.vector.tensor_scalar(out=ca[:, a, :], in0=w[:, a, 0:128], scalar1=cw_sb[:, a * 5:a * 5 + 1], scalar2=None, op0=ALU.mult),
                         reads=[wn, "cw"], writes=[can + "_%d" % a])
                else:
                    S.op("dve", lambda: nc.vector.scalar_tensor_tensor(out=ca[:, a, :], in0=w[:, a, j:j + 128], scalar=cw_sb[:, a * 5 + j:a * 5 + j + 1],
                                                                        in1=ca[:, a, :], op0=ALU.mult, op1=ALU.add),
                         reads=[wn, "cw", can + "_%d" % a], writes=[can + "_%d" % a])
        can3 = [can + "_%d" % a for a in range(3)]
        if nv < 128:
            S.op("dve", lambda: nc.vector.memset(ca[:, :, nv:128], 0.0), reads=[], writes=can3)
        S.op("act", lambda: nc.scalar.activation(out=qkv_s[:], in_=ca[:], func=AF.Silu), reads=can3, writes=["qkv_s"])
        S.op("pool", lambda: nc.gpsimd.tensor_tensor(out=sq[:], in0=qkv_s[:, 0:2, :], in1=qkv_s[:, 0:2, :], op=ALU.mult), reads=["qkv_s"], writes=["sq"])
        ps, pn = fbank()
        S.op("pe", lambda: nc.tensor.matmul(ps[:, 0:256], lhsT=ones_b[:], rhs=sq[:].rearrange("p a t -> p (a t)"), start=True, stop=True),
             reads=["ones_b", "sq"], writes=[pn])
        S.op("act", lambda: nc.scalar.activation(out=rs[:].rearrange("p a t -> p (a t)"), in_=ps[:, 0:256], func=AF.Sqrt, bias=l2eps[:], scale=1.0),
             reads=[pn, "l2eps"], writes=["rs"])
        S.op("dve", lambda: nc.vector.reciprocal(out=rs[:], in_=rs[:]), reads=["rs"], writes=["rs"])
        S.op("dve", lambda: nc.vector.scalar_tensor_tensor(out=qT[slot][:], in0=qkv_s[:, 0, :], scalar=128.0 ** -0.5, in1=rs[:, 0, :], op0=ALU.mult, op1=ALU.mult),
             reads=["qkv_s", "rs"], writes=["qT%d" % slot])
        S.op("dve", lambda: nc.vector.tensor_tensor(out=kT[slot][:], in0=qkv_s[:, 1, :], in1=rs[:, 1, :], op=ALU.mult),
             reads=["qkv_s", "rs"], writes=["kT%d" % slot])
        S.op("pool", lambda: nc.gpsimd.tensor_copy(out=vT[slot][:], in_=qkv_s[:, 2, :]), reads=["qkv_s"], writes=["vT%d" % slot])
        pb, pbn = bbank()
        S.op("pe", lambda: nc.tensor.transpose(out=pb[:, 0:128], in_=kT[slot][:], identity=ident_b[:]), reads=["kT%d" % slot, "ident_b"], writes=[pbn], inc=False)
        S.op("pe", lambda: nc.tensor.transpose(out=pb[:, 128:256], in_=vT[slot][:], identity=ident_b[:]), reads=["vT%d" % slot, "ident_b"], writes=[pbn])
        S.op("act", lambda: nc.scalar.copy(out=ktok[slot][:], in_=pb[:, 0:128]), reads=[pbn], writes=["ktok%d" % slot])
        S.op("act", lambda: nc.scalar.copy(out=vtok[slot][:], in_=pb[:, 128:256]), reads=[pbn], writes=["vtok%d" % slot])
        S.dma(None, None, reads=["ba_b%d" % (t0 // 512)], writes=["batok%d" % slot],
              fn=lambda: S.eng[S.defq].dma_start(out=batok[slot][:], in_=BA[:, t0:t0 + 128].rearrange("a t -> t a"), allow_slow_non_contiguous=True))
        bn = "batok%d" % slot
        S.op("act", lambda: nc.scalar.activation(out=beta[slot][:], in_=batok[slot][:, 0:2], func=AF.Sigmoid), reads=[bn], writes=["beta%d" % slot])
        S.op("dve", lambda: nc.vector.tensor_scalar(out=nbeta[slot][:], in0=beta[slot][:], scalar1=-1.0, scalar2=None, op0=ALU.mult),
             reads=["beta%d" % slot], writes=["nbeta%d" % slot])
        gt = gtok[slot]
        gn = "gtok%d" % slot
        S.op("dve", lambda: nc.vector.tensor_tensor(out=gt[:], in0=batok[slot][:, 2:4], in1=dtb_sb[:], op=ALU.add), reads=[bn, "dtb"], writes=[gn])
        S.op("act", lambda: nc.scalar.activation(out=gt[:], in_=gt[:], func=AF.Exp), reads=[gn], writes=[gn])
        S.op("act", lambda: nc.scalar.activation(out=gt[:], in_=gt[:], func=AF.Ln, bias=1.0, scale=1.0), reads=[gn], writes=[gn])
        S.op("dve", lambda: nc.vector.tensor_tensor(out=gt[:], in0=gt[:], in1=negA[:], op=ALU.mult), reads=[gn, "negA"], writes=[gn])
        hn_ = "ghl%d" % slot
        S.op("dve", lambda: nc.vector.tensor_copy(out=ghb[slot][:], in_=gt[:]), reads=[gn], writes=["ghb%d" % slot])
        S.op("dve", lambda: nc.vector.tensor_copy(out=ghl[slot][:, 0, :], in_=ghb[slot][:]), reads=["ghb%d" % slot], writes=[hn_])
        S.op("dve", lambda: nc.vector.tensor_tensor(out=ghl[slot][:, 1, :], in0=gt[:], in1=ghl[slot][:, 0, :], op=ALU.subtract), reads=[gn, hn_], writes=[hn_])

    ND = 4
    gA = [sb("gA%d" % i, [128, 2, 128], BF16) for i in range(ND)]
    gB = [sb("gB%d" % i, [128, 2, 128], BF16) for i in range(ND)]
    gO = [sb("gO%d" % i, [128, 2, 128], BF16) for i in range(ND)]
    gcc = [sb("gcc%d" % i, [128, 2], F32) for i in range(ND)]
    egc = [sb("egc%d" % i, [128, 3], F32) for i in range(ND)]
    DL = [sb("DL%d" % i, [128, 128], F32) for i in range(ND)]
    DQ = [sb("DQ%d" % i, [128, 128], F32) for i in range(ND)]
    egrow = [sb("egrow%d" % i, [128, 128], F32) for i in range(ND)]
    Pm = [[sb("P%d_%d" % (i, j), [128, 128], BF16) for j in range(2)] for i in range(ND)]
    PTm = [[sb("PT%d_%d" % (i, j), [128, 128], BF16) for j in range(2)] for i in range(ND)]
    Xm = [[sb("X%d_%d" % (i, j), [128, 128], BF16) for j in range(2)] for i in range(ND)]
    QKm = [sb("QKm%d" % i, [128, 128], BF16) for i in range(ND)]
    qdec = [sb("qdec%d" % i, [128, 128], BF16) for i in range(ND)]
    kdec = [sb("kdec%d" % i, [128, 128], BF16) for i in range(ND)]
    kb2 = [sb("kb2%d" % i, [128, 128], BF16) for i in range(ND)]
    bv = [sb("bv%d" % i, [128, 128], BF16) for i in range(ND)]
    u_sb = [sb("u%d" % i, [128, 128], F32) for i in range(ND)]
    wT = [sb("wT%d" % i, [128, 128], BF16) for i in range(ND)]
    vnew = [sb("vnew%d" % i, [128, 128], BF16) for i in range(ND)]
    Sf = [sb("Sf%d" % i, [128, 128], F32) for i in range(3)]
    Sb = [sb("Sb%d" % i, [128, 128], BF16) for i in range(3)]
    max_ch = max(Tp for (_, _, Tp) in lay) // 128
    obuf0 = sb("obuf", [128, max_ch, 128], BF16)
    min_ch = max([Tp for (_, _, Tp) in lay[1:]] + [128]) // 128
    obuf1 = sb("obuf1", [128, min_ch, 128], BF16)
    obufs = [obuf0, obuf1]
    ofin = sb("ofin", [128, 128], F32)
    osq = sb("osq", [128, 128], F32)
    oss = sb("oss", [128, 1], F32)
    onb = sb("onb", [128, 128], BF16)
    zas_c = sb("zas_c", [128, 128], BF16)
    ya_c = [sb("ya_c%d" % i, [128, 128], BF16) for i in range(2)]

    def send_rows(tok0, n):
        j = tok0 // TC
        off = tok0 - j * TC
        return j, off

    def dir_ops(slot, di, dr):
        hn_ = "ghl%d" % slot
        n = lambda s: "%s%d" % (s, di)
        for h in range(2):
            gcol = ghl[slot][:, h, dr:dr + 1]
            S.op("pool", lambda: nc.gpsimd.tensor_scalar(out=gA[di][:, h, :], in0=maskA[:, dr, 0:128], scalar1=gcol, scalar2=None, op0=ALU.mult),
                 reads=["maskA", hn_], writes=[n("gA")])
            S.op("pool", lambda: nc.gpsimd.tensor_scalar(out=gB[di][:, h, :], in0=maskB[:, dr, 0:128], scalar1=gcol, scalar2=None, op0=ALU.mult),
                 reads=["maskB", hn_], writes=[n("gB")])
            S.op("pool", lambda: nc.gpsimd.tensor_scalar(out=gO[di][:, h, :], in0=ones_f[:], scalar1=gcol, scalar2=None, op0=ALU.mult),
                 reads=["ones_f", hn_], writes=[n("gO")])
        if stop == "D0":
            raise StopBuild()
        ps, pn = fbank()
        for h in range(2):
            S.op("pe", lambda: nc.tensor.matmul(ps[:, 0:128], lhsT=gA[di][:, h, :], rhs=maskB_b[:, dr, 0:128], start=(h == 0), stop=False),
                 reads=[n("gA"), "maskB_b"], writes=[pn], inc=False)
        KV = stop if (stop or "").startswith("KV") else None
        S.op("pe", lambda: nc.tensor.matmul(ps[:, 0:128], lhsT=ident_b[:], rhs=negL[:, dr, :], start=False, stop=True),
             reads=["ident_b", "negL"], writes=[pn], inc=(KV == "KV1"))
        if KV == "KV1":
            raise StopBuild()
        for h in range(2):
            S.op("pe", lambda: nc.tensor.matmul(ps[:, 256:272], lhsT=gA[di][:, h, :], rhs=maskB_b[:, dr, 128:144], start=(h == 0), stop=(h == 1)),
                 reads=[n("gA"), "maskB_b"], writes=[pn], inc=(h == 1))
        if KV == "KV2":
            raise StopBuild()
        S.op("act", lambda: nc.scalar.activation(out=DL[di][:], in_=ps[:, 0:128], func=AF.Exp), reads=[pn], writes=[n("DL")])
        if KV == "KV3":
            raise StopBuild()
        S.op("dve", lambda: nc.vector.tensor_copy(out=gcc[di][:, 0:1], in_=ps[:, 256:257]), reads=[pn], writes=[n("gcc")])
        if stop == "D0b":
            raise StopBuild()
        ps2, pn2 = fbank()
        for h in range(2):
            S.op("pe", lambda: nc.tensor.matmul(ps2[:, 0:128], lhsT=gB[di][:, h, :], rhs=maskA_b[:, dr, 0:128], start=(h == 0), stop=False),
                 reads=[n("gB"), "maskA_b"], writes=[pn2], inc=False)
        S.op("pe", lambda: nc.tensor.matmul(ps2[:, 0:128], lhsT=ident_b[:], rhs=negQ[:, dr, :], start=False, stop=True),
             reads=["ident_b", "negQ"], writes=[pn2])
        S.op("act", lambda: nc.scalar.activation(out=DQ[di][:], in_=ps2[:, 0:128], func=AF.Exp), reads=[pn2], writes=[n("DQ")])
        ps3, pn3 = fbank()
        for h in range(2):
            S.op("pe", lambda: nc.tensor.matmul(ps3[:, 0:128], lhsT=gO[di][:, h, :], rhs=maskA_b[:, dr, 0:128], start=(h == 0), stop=(h == 1)),
                 reads=[n("gO"), "maskA_b"], writes=[pn3], inc=False)
        for h in range(2):
            S.op("pe", lambda: nc.tensor.matmul(ps3[:, 256:272], lhsT=gO[di][:, h, :], rhs=maskA_b[:, dr, 128:144], start=(h == 0), stop=(h == 1)),
                 reads=[n("gO"), "maskA_b"], writes=[pn3], inc=(h == 1))
        S.op("act", lambda: nc.scalar.activation(out=egrow[di][:], in_=ps3[:, 0:128], func=AF.Exp), reads=[pn3], writes=[n("egrow")])
        S.op("dve", lambda: nc.vector.tensor_copy(out=gcc[di][:, 1:2], in_=ps3[:, 256:257]), reads=[pn3], writes=[n("gcc")])
        S.op("act", lambda: nc.scalar.activation(out=egc[di][:, 0:1], in_=gcc[di][:, 0:1], func=AF.Exp), reads=[n("gcc")], writes=[n("egc")])
        S.op("act", lambda: nc.scalar.activation(out=egc[di][:, 1:2], in_=gcc[di][:, 0:1], func=AF.Exp, bias=gcc[di][:, 1:2], scale=-1.0),
             reads=[n("gcc")], writes=[n("egc")])
        S.op("act", lambda: nc.scalar.activation(out=egc[di][:, 2:3], in_=gcc[di][:, 1:2], func=AF.Exp), reads=[n("gcc")], writes=[n("egc")])
        if stop == "D1":
            raise StopBuild()
        psg, png = fbank()
        S.op("pe", lambda: nc.tensor.matmul(psg[:, 0:128], lhsT=kT[slot][:], rhs=kT[slot][:], start=True, stop=True), reads=["kT%d" % slot], writes=[png], inc=False)
        S.op("pe", lambda: nc.tensor.matmul(psg[:, 128:256], lhsT=kT[slot][:], rhs=qT[slot][:], start=True, stop=True),
             reads=["kT%d" % slot, "qT%d" % slot], writes=[png])
        S.op("dve", lambda: nc.vector.scalar_tensor_tensor(out=PTm[di][0][:], in0=psg[:, 0:128], scalar=nbeta[slot][:, dr:dr + 1], in1=DL[di][:],
                                                            op0=ALU.mult, op1=ALU.mult),
             reads=[png, "nbeta%d" % slot, n("DL")], writes=[n("PT") + "_0"])
        S.op("dve", lambda: nc.vector.tensor_tensor(out=QKm[di][:], in0=psg[:, 128:256], in1=DQ[di][:], op=ALU.mult), reads=[png, n("DQ")], writes=[n("QKm")])
        pb, pbn = bbank()
        S.op("pe", lambda: nc.tensor.transpose(out=pb[:, 0:128], in_=PTm[di][0][:], identity=ident_b[:]), reads=[n("PT") + "_0", "ident_b"], writes=[pbn])
        S.op("act", lambda: nc.scalar.copy(out=Pm[di][0][:], in_=pb[:, 0:128]), reads=[pbn], writes=[n("P") + "_0"])
        S.op("dve", lambda: nc.vector.tensor_tensor(out=Xm[di][0][:], in0=pb[:, 0:128], in1=ident_b[:], op=ALU.add), reads=[pbn, "ident_b"], writes=[n("X") + "_0"])
        if stop == "D2":
            raise StopBuild()
        for k in range(1, 7):
            a, b = (k - 1) % 2, k % 2
            psd, pnd = fbank()
            S.op("pe", lambda: nc.tensor.matmul(psd[:, 0:128], lhsT=Pm[di][a][:], rhs=PTm[di][a][:], start=True, stop=True),
                 reads=[n("P") + "_%d" % a, n("PT") + "_%d" % a], writes=[pnd], inc=(k == 6))
            if k < 6:
                S.op("pe", lambda: nc.tensor.matmul(psd[:, 128:256], lhsT=PTm[di][a][:], rhs=Pm[di][a][:], start=True, stop=True),
                     reads=[n("P") + "_%d" % a, n("PT") + "_%d" % a], writes=[pnd])
            S.op("act", lambda: nc.scalar.copy(out=PTm[di][b][:], in_=psd[:, 0:128]), reads=[pnd], writes=[n("PT") + "_%d" % b])
            if k < 6:
                S.op("dve", lambda: nc.vector.tensor_copy(out=Pm[di][b][:], in_=psd[:, 128:256]), reads=[pnd], writes=[n("P") + "_%d" % b])
            psx, pnx = fbank()
            S.op("pe", lambda: nc.tensor.matmul(psx[:, 0:128], lhsT=PTm[di][b][:], rhs=Xm[di][a][:], start=True, stop=True),
                 reads=[n("PT") + "_%d" % b, n("X") + "_%d" % a], writes=[pnx])
            S.op("dve", lambda: nc.vector.tensor_tensor(out=Xm[di][b][:], in0=psx[:, 0:128], in1=Xm[di][a][:], op=ALU.add),
                 reads=[pnx, n("X") + "_%d" % a], writes=[n("X") + "_%d" % b])
        if stop == "D3":
            raise StopBuild()
        Xf = Xm[di][0]
        Xfn = n("X") + "_0"
        S.op("pool", lambda: nc.gpsimd.tensor_scalar(out=bv[di][:], in0=vtok[slot][:], scalar1=beta[slot][:, dr:dr + 1], scalar2=None, op0=ALU.mult),
             reads=["vtok%d" % slot, "beta%d" % slot], writes=[n("bv")])
        S.op("dve", lambda: nc.vector.tensor_scalar(out=kb2[di][:], in0=ktok[slot][:], scalar1=beta[slot][:, dr:dr + 1], scalar2=egc[di][:, 0:1], op0=ALU.mult, op1=ALU.mult),
             reads=["ktok%d" % slot, "beta%d" % slot, n("egc")], writes=[n("kb2")])
        S.op("pool", lambda: nc.gpsimd.tensor_scalar(out=kdec[di][:], in0=ktok[slot][:], scalar1=egc[di][:, 1:2], scalar2=None, op0=ALU.mult),
             reads=["ktok%d" % slot, n("egc")], writes=[n("kdec")])
        S.op("pool", lambda: nc.gpsimd.tensor_tensor(out=qdec[di][:], in0=qT[slot][:], in1=egrow[di][:], op=ALU.mult),
             reads=["qT%d" % slot, n("egrow")], writes=[n("qdec")])
        psu, pnu = fbank()
        S.op("pe", lambda: nc.tensor.matmul(psu[:, 0:128], lhsT=Xf[:], rhs=bv[di][:], start=True, stop=True), reads=[Xfn, n("bv")], writes=[pnu], inc=False)
        S.op("pe", lambda: nc.tensor.matmul(psu[:, 128:256], lhsT=kb2[di][:], rhs=Xf[:], start=True, stop=True), reads=[Xfn, n("kb2")], writes=[pnu])
        S.op("act", lambda: nc.scalar.copy(out=u_sb[di][:], in_=psu[:, 0:128]), reads=[pnu], writes=[n("u")])
        S.op("dve", lambda: nc.vector.tensor_copy(out=wT[di][:], in_=psu[:, 128:256]), reads=[pnu], writes=[n("wT")])

    def scan_step(c, di, dr, first, st=None, ob=0):
        st = dr if st is None else st
        obuf = obufs[ob]
        n = lambda s: "%s%d" % (s, di)
        Sn, Sbn = "Sf%d" % st, "Sb%d" % st
        if first:
            S.op("pool", lambda: nc.gpsimd.memset(Sf[st][:], 0.0), writes=[Sn])
            S.op("pool", lambda: nc.gpsimd.memset(Sb[st][:], 0.0), writes=[Sbn])
        ps, pn = fbank()
        S.op("pe", lambda: nc.tensor.matmul(ps[:, 0:128], lhsT=wT[di][:], rhs=Sb[st][:], start=True, stop=True), reads=[n("wT"), Sbn], writes=[pn])
        S.op("dve", lambda: nc.vector.scalar_tensor_tensor(out=vnew[di][:], in0=ps[:, 0:128], scalar=-1.0, in1=u_sb[di][:], op0=ALU.mult, op1=ALU.add),
             reads=[pn, n("u")], writes=[n("vnew")])
        pso, pno = fbank()
        S.op("pe", lambda: nc.tensor.matmul(pso[:, 0:128], lhsT=qdec[di][:], rhs=Sb[st][:], start=True, stop=False), reads=[n("qdec"), Sbn], writes=[pno], inc=False)
        S.op("pe", lambda: nc.tensor.matmul(pso[:, 0:128], lhsT=QKm[di][:], rhs=vnew[di][:], start=False, stop=True), reads=[n("QKm"), n("vnew")], writes=[pno], inc=False)
        S.op("pe", lambda: nc.tensor.matmul(pso[:, 128:256], lhsT=kdec[di][:], rhs=vnew[di][:], start=True, stop=True), reads=[n("kdec"), n("vnew")], writes=[pno])
        S.op("dve", lambda: nc.vector.scalar_tensor_tensor(out=Sf[st][:], in0=Sf[st][:], scalar=egc[di][:, 2:3], in1=pso[:, 128:256], op0=ALU.mult, op1=ALU.add),
             reads=[pno, n("egc"), Sn], writes=[Sn])
        S.op("act", lambda: nc.scalar.copy(out=Sb[st][:], in_=Sf[st][:]), reads=[Sn], writes=[Sbn])
        on = "obuf%d_%d" % (ob, c)
        if dr == 0:
            S.op("act", lambda: nc.scalar.copy(out=obuf[:, c, :], in_=pso[:, 0:128]), reads=[pno], writes=[on])
        else:
            S.op("dve", lambda: nc.vector.tensor_tensor(out=ofin[:], in0=pso[:, 0:128], in1=obuf[:, c, :], op=ALU.add), reads=[pno, on], writes=["ofin"])

    def finish_chunk(si, c):
        base, T, Tp = lay[si]
        t0 = base + 128 * c
        S.op("dve", lambda: nc.vector.tensor_tensor(out=osq[:], in0=ofin[:], in1=ofin[:], op=ALU.mult), reads=["ofin"], writes=["osq"])
        S.op("dve", lambda: nc.vector.tensor_reduce(out=oss[:], in_=osq[:], axis=mybir.AxisListType.X, op=ALU.add), reads=["osq"], writes=["oss"])
        S.op("act", lambda: nc.scalar.activation(out=oss[:], in_=oss[:], func=AF.Sqrt, bias=rmseps[:], scale=1.0 / 128.0), reads=["oss", "rmseps"], writes=["oss"])
        S.op("dve", lambda: nc.vector.reciprocal(out=oss[:], in_=oss[:]), reads=["oss"], writes=["oss"])
        S.op("dve", lambda: nc.vector.scalar_tensor_tensor(out=onb[:], in0=ofin[:], scalar=oss[:], in1=dnw_sb[:], op0=ALU.mult, op1=ALU.mult),
             reads=["ofin", "oss", "dnw"], writes=["onb"])
        pb, pbn = bbank()
        S.op("pe", lambda: nc.tensor.transpose(out=pb[:, 0:128], in_=onb[:], identity=ident_b[:]), reads=["onb", "ident_b"], writes=[pbn])
        S.dma(zas_c[:], ZAS[:, t0:t0 + 128], reads=["zas_b%d" % (t0 // 512)], writes=["zas_c"])
        yc = ya_c[c % 2]
        ycn = "ya_c%d" % (c % 2)
        S.op("dve", lambda: nc.vector.tensor_tensor(out=yc[:], in0=pb[:, 0:128], in1=zas_c[:], op=ALU.mult), reads=[pbn, "zas_c"], writes=[ycn])
        j, off = send_rows(t0, 128)
        S.dma(SEND[j * 256:j * 256 + 128, off:off + 128], yc[:], reads=[ycn, "send0"], writes=["send"], multi=True)
        if debug:
            S.dma(YDBG[0:128, t0:t0 + 128], yc[:], reads=[ycn], writes=["ydbg"], multi=True)

    def seq_steps(si):
        base, T, Tp = lay[si]
        nch = Tp // 128
        out_ = []
        for dr in (0, 1):
            order = range(nch) if dr == 0 else range(nch - 1, -1, -1)
            for idx, c in enumerate(order):
                out_.append((si, c, dr, idx))
        return out_
    st0 = seq_steps(0)
    n0 = lay[0][2] // 128
    sA, sB = st0[:n0], st0[n0:]
    sC = [x for si in range(1, len(lay)) for x in seq_steps(si)]
    sched = [(0, x) for x in sA]
    ia = ib = 0
    while ia < len(sB) or ib < len(sC):
        if ia < len(sB):
            sched.append((0, sB[ia]))
            ia += 1
        if ib < len(sC):
            sched.append((1, sC[ib]))
            ib += 1
    cnt_ch = [0, 0]
    for ch, (si, c, dr, idx) in sched:
        base, T, Tp = lay[si]
        slot = 2 * ch + cnt_ch[ch] % 2
        di = slot
        cnt_ch[ch] += 1
        if dr == 0:
            prep_chunk(si, c, slot)
            spill_chunk(base // 128 + c, slot)
        else:
            reload_chunk(base // 128 + c, slot)
        try:
            dir_ops(slot, di, dr)
        except StopBuild:
            S.finish("sp")
            return nc, dict(lay=lay, TG=TG, TC=TC, NTC=NTC, fac=fac, nins=S.nins)
        scan_step(c, di, dr, idx == 0, st=(dr if ch == 0 else 2), ob=ch)
        if dr == 1:
            finish_chunk(si, c)

    if stop == "B1":
        S.finish("sp")
        return nc, dict(lay=lay, TG=TG, TC=TC, NTC=NTC, fac=fac, nins=S.nins)
    n1m = max(fac[T][0] for T in ftypes)
    n2m = max(fac[T][1] for T in ftypes)
    npm = -(-n2m // 128)
    Tm = max(ftypes)
    c1_sb = {T: sb("c1sb_%d" % T, [fac[T][0], 3, fac[T][0]], BF16) for T in ftypes}
    c1_32 = sb("c1_32", [n1m, 3, n1m], F32)
    for T in ftypes:
        n1_ = fac[T][0]
        S.dma(c1_32[0:n1_, :, 0:n1_], tabs[T]["c1"], writes=["c1_32"])
        S.op("pool", lambda: nc.gpsimd.tensor_copy(out=c1_sb[T][:], in_=c1_32[0:n1_, :, 0:n1_]), reads=["c1_32"], writes=["c1sb_%d" % T])
    ZCH = 8
    zin = [sb("zin%d" % i, [n1m, ZCH, 256], BF16) for i in range(2)]
    yst = [sb("yst%d" % i, [n1m, ZCH, 256], BF16) for i in range(2)]
    KG = 4
    y2 = [sb("y2_%d" % i, [128, npm, KG, 256], BF16) for i in range(2)]
    gt_sb = [sb("gt%d" % i, [128, npm, 2, KG, n2m], BF16) for i in range(2)]
    g32 = sb("g32", [128, npm, 2, KG, n2m], F32)
    zfs_sb = sb("zfs_sb", [128, Tm], BF16)
    fcnt = 0
    gi = 0
    for si, (base, T, Tp) in enumerate(lay):
        n1, n2 = fac[T]
        npass = -(-n2 // 128)
        c1t = c1_sb[T]
        c1n = "c1sb_%d" % T
        blks_all = ["z_b%d" % b for b in range(base // 512, (base + T - 1) // 512 + 1)]
        zsrc = Z[base:base + T, :].rearrange("(a b) n -> a b n", b=n2)
        ybv = YB[0:n1 * n2 * 256].rearrange("(a b n) -> a b n", b=n2, n=256)
        for s0 in range(0, n2, ZCH):
            sl = min(ZCH, n2 - s0)
            zi_ = zin[fcnt % 2]
            zn = "zin%d" % (fcnt % 2)
            yo = yst[fcnt % 2]
            yn = "yst%d" % (fcnt % 2)
            fcnt += 1
            S.dma(zi_[0:n1, 0:sl, :], zsrc[:, s0:s0 + sl, :], reads=blks_all, writes=[zn])
            for g0 in range(0, sl, 4):
                gl = min(4, sl - g0)
                for part in range(2):
                    ps, pn = fbank()
                    if part == 0:
                        terms = [(0, 0), (1, 1)]
                    else:
                        terms = [(0, 1), (2, 0)]
                    for ti, (tab, comp) in enumerate(terms):
                        S.op("pe", lambda: nc.tensor.matmul(ps[0:n1, 0:gl * 128].rearrange("p (a b) -> p a b", a=gl), lhsT=c1t[:, tab, :],
                                                             rhs=zi_[0:n1, g0:g0 + gl, comp * 128:(comp + 1) * 128], start=(ti == 0), stop=(ti == 1)),
                             reads=[c1n, zn], writes=[pn], inc=(ti == 1))
                    if part == 0:
                        S.op("dve", lambda: nc.vector.tensor_copy(out=yo[0:n1, g0:g0 + gl, part * 128:(part + 1) * 128],
                                                                   in_=ps[0:n1, 0:gl * 128].rearrange("p (a b) -> p a b", a=gl)), reads=[pn], writes=[yn])
                    else:
                        S.op("act", lambda: nc.scalar.copy(out=yo[0:n1, g0:g0 + gl, part * 128:(part + 1) * 128],
                                                            in_=ps[0:n1, 0:gl * 128].rearrange("p (a b) -> p a b", a=gl)), reads=[pn], writes=[yn])
            S.dma(ybv[:, s0:s0 + sl, :], yo[0:n1, 0:sl, :], reads=[yn], writes=["yb"], multi=True)
        zblks = ["zfs_b%d" % b for b in range(base // 512, (base + T - 1) // 512 + 1)]
        S.dma(zfs_sb[:, 0:T], ZFS[:, base:base + T], reads=zblks, writes=["zfs_sb"])
        zv = zfs_sb[:, 0:T].rearrange("p (k2 k1) -> p k1 k2", k1=n1)
        norm = 1.0 / math.sqrt(T * 256.0)
        kg = max(1, min(KG, 512 // n2))
        gtab = tabs[T]["g"]
        for k0 in range(0, n1, kg):
            kl = min(kg, n1 - k0)
            gtb = gt_sb[gi % 2]
            gtn = "gt%d" % (gi % 2)
            y2b = y2[gi % 2]
            y2n = "y2_%d" % (gi % 2)
            gi += 1
            for p_ in range(npass):
                pl = min(128, n2 - p_ * 128)
                for comp in range(2):
                    S.dma(g32[0:pl, p_, comp, 0:kl, 0:n2], gtab[p_ * 128:p_ * 128 + pl, comp, k0:k0 + kl, :], reads=[], writes=["g32_%d" % p_], multi=(comp > 0))
                S.op("pool", lambda: nc.gpsimd.tensor_copy(out=gtb[0:pl, p_, :, 0:kl, 0:n2], in_=g32[0:pl, p_, :, 0:kl, 0:n2]),
                     reads=["g32_%d" % p_], writes=[gtn + "_%d" % p_])
                S.dma(y2b[0:pl, p_, 0:kl, :], ybv[k0:k0 + kl, p_ * 128:p_ * 128 + pl, :].rearrange("a b n -> b a n"), reads=["yb"], writes=[y2n], multi=(p_ > 0))
            ps, pn = fbank()
            for kk in range(kl):
                nmm = 2 * npass
                mi = 0
                for p_ in range(npass):
                    pl = min(128, n2 - p_ * 128)
                    for comp in range(2):
                        S.op("pe", lambda: nc.tensor.matmul(ps[:, kk * n2:(kk + 1) * n2], lhsT=y2b[0:pl, p_, kk, comp * 128:(comp + 1) * 128],
                                                             rhs=gtb[0:pl, p_, comp, kk, 0:n2], start=(mi == 0), stop=(mi == nmm - 1)),
                             reads=[y2n, gtn + "_%d" % p_], writes=[pn], inc=(mi == nmm - 1 and kk == kl - 1))
                        mi += 1
            S.op("dve", lambda: nc.vector.scalar_tensor_tensor(out=zv[:, k0:k0 + kl, :], in0=ps[:, 0:kl * n2].rearrange("p (a b) -> p a b", a=kl), scalar=norm,
                                                                in1=zv[:, k0:k0 + kl, :], op0=ALU.mult, op1=ALU.mult),
                 reads=[pn, "zfs_sb"], writes=["zfs_sb"])
        S.bufs["yb"]["w"] = S._compress(S.bufs["yb"]["w"] + S.bufs["yb"]["r"])
        S.bufs["yb"]["r"] = []
        t = base
        while t < base + T:
            j = t // TC
            e_ = min(base + T, (j + 1) * TC)
            S.dma(SEND[j * 256 + 128:j * 256 + 256, t - j * TC:e_ - j * TC], zfs_sb[:, t - base:e_ - base], reads=["zfs_sb", "send0"], writes=["send"], multi=True)
            if debug:
                S.dma(YDBG[128:256, t:e_], zfs_sb[:, t - base:e_ - base], reads=["zfs_sb"], writes=["ydbg"], multi=True)
            t = e_

    if stop == "B":
        S.finish("sp")
        return nc, dict(lay=lay, TG=TG, TC=TC, NTC=NTC, fac=fac, nins=S.nins)
    S.barrier()
    esB.close()
    esC = ExitStack()
    es_cur[0] = esC
    import os as _os
    for _i in range(int(_os.environ.get("KDELAY", "0"))):
        S.op("dve", lambda: nc.vector.memset(zero_t[:], 0.0), writes=["zero_t"])
    for _i in range(int(_os.environ.get("KDMA", "0"))):
        S.dma(SEND[0:1, 0:16], zero_t[0:1, 0:16], reads=["zero_t"], writes=["dummy"], multi=True, q=_os.environ.get("KQ", "sp"))
    if int(_os.environ.get("KDELAY", "0")) or int(_os.environ.get("KDMA", "0")):
        S.barrier()
    ccsem = nc.alloc_semaphore("ccsem")
    if not _os.environ.get("KNOCC"):
        nc.gpsimd.collective_compute("AllGather", ALU.bypass, replica_groups=[list(range(NCORE))], ins=[SEND.opt()], outs=[RECV.opt()]).then_inc(ccsem)
        nc.gpsimd.wait_ge(ccsem, 1)
    S.defq = "sp"
    gidx = sb("gidx", [128, NTC, 16], I32)
    S.dma(gidx[:], gidx_in, writes=["gidx"], q="pool")
    RECV_v = RECV.rearrange("r (t c) -> (r t) c", c=128)
    if stop == "X":
        S.finish("sp")
        return nc, dict(lay=lay, TG=TG, TC=TC, NTC=NTC, fac=fac, nins=S.nins)

    NT2 = 2
    bc = sb("bc", [128, 4, D_MODEL], F32)
    S.dma(bc[:], bc_in, writes=["bc"])
    bg_sb = sb("bg_sb", [1, 4096], BF16)
    S.dma(bg_sb[:], bg_in, writes=["bg_sb"])
    yT = [sb("yT%d" % i, [128, 16, 128], BF16) for i in range(2 * NT2)]
    xct = [sb("xct%d" % i, [128, D_MODEL], F32) for i in range(2)]
    wgs = [sb("wgs%d" % i, [128, KD, 512], BF16) for i in range(2)]
    hTc = [sb("hTc%d" % i, [128, KD, 128], BF16) for i in range(NT2)]
    hres = [sb("hres%d" % i, [128, D_MODEL], F32) for i in range(NT2)]
    hbc = sb("hbc", [128, D_MODEL], BF16)
    gate = [sb("gate%d" % i, [128, 4096], BF16) for i in range(NT2)]
    mrg = [sb("mrg%d" % i, [128, D_MODEL], BF16) for i in range(NT2)]
    mtmp = sb("mtmp", [128, 512], F32)
    mb = [sb("mb%d" % i, [128, D_MODEL], BF16) for i in range(NT2)]
    mT = [sb("mT%d" % i, [128, KD, 128], BF16) for i in range(NT2)]
    st6c = sb("st6c", [128, 4, 6], F32)
    mvc = sb("mvc", [128, 2], F32)
    rstdc = sb("rstdc", [128, 1], F32)
    nmrc = sb("nmrc", [128, 1], F32)
    epsc = sb("epsc", [128, 1], F32)
    S.op("pool", lambda: nc.gpsimd.memset(epsc[:], LN_EPS), writes=["epsc"])
    wg_v = WG16.rearrange("(k p) n -> p k n", p=128)
    wpa_v = WPA16.rearrange("(k p) n -> p k n", p=128)
    wpf_v = WPF16.rearrange("(k p) n -> p k n", p=128)
    wo_v = WO16.rearrange("(k p) n -> p k n", p=128)
    wcnt = [0]

    def load_w(view, kn, c0, rname):
        i = wcnt[0] % 2
        wcnt[0] += 1
        S.dma(wgs[i][:, 0:kn, :], view[:, :, c0:c0 + 512], reads=[rname], writes=["wgs%d" % i])
        return wgs[i], "wgs%d" % i

    def ln_stats(src, srcn):
        for c in range(4):
            S.op("dve", lambda: nc.vector.bn_stats(out=st6c[:, c, :], in_=src[:, c * 512:(c + 1) * 512]), reads=[srcn], writes=["st6c"])
        S.op("dve", lambda: nc.vector.bn_aggr(out=mvc[:], in_=st6c[:]), reads=["st6c"], writes=["mvc"])
        S.op("act", lambda: nc.scalar.activation(out=rstdc[:], in_=mvc[:, 1:2], func=AF.Sqrt, bias=epsc[:], scale=1.0), reads=["mvc", "epsc"], writes=["rstdc"])
        S.op("dve", lambda: nc.vector.reciprocal(out=rstdc[:], in_=rstdc[:]), reads=["rstdc"], writes=["rstdc"])
        S.op("dve", lambda: nc.vector.scalar_tensor_tensor(out=nmrc[:], in0=mvc[:, 0:1], scalar=-1.0, in1=rstdc[:], op0=ALU.mult, op1=ALU.mult),
             reads=["mvc", "rstdc"], writes=["nmrc"])

    xcnt = 0
    ycnt = 0
    for t0 in range(0, NTC, NT2):
        tiles = list(range(t0, min(NTC, t0 + NT2)))
        yts = {}
        for li, tt in enumerate(tiles):
            xi = xcnt % 2
            xcnt += 1
            xn = "xct%d" % xi
            S.dma(xct[xi][:], xc[tt * 128:(tt + 1) * 128, :], writes=[xn])
            yi = ycnt % (2 * NT2)
            ycnt += 1
            yts[li] = (yT[yi], "yT%d" % yi)
            for r in range(16):
                S.dma(None, None, reads=["gidx"], writes=["yT%d" % yi], q="pool", multi=(r > 0),
                      fn=lambda: nc.gpsimd.indirect_dma_start(out=yT[yi][:, r, :], out_offset=None, in_=RECV_v,
                                                               in_offset=bass.IndirectOffsetOnAxis(ap=gidx[:, tt, r:r + 1], axis=0)))
            ln_stats(xct[xi], xn)
            hr = hres[li]
            hn = "hres%d" % li
            S.op("act", lambda: nc.scalar.activation(out=hr[:], in_=xct[xi][:], func=AF.Identity, bias=nmrc[:], scale=rstdc[:]),
                 reads=[xn, "nmrc", "rstdc"], writes=[hn])
            S.op("dve", lambda: nc.vector.tensor_tensor(out=hr[:], in0=hr[:], in1=bc[:, 0, :], op=ALU.mult), reads=[hn, "bc"], writes=[hn])
            S.op("pool", lambda: nc.gpsimd.tensor_tensor(out=hr[:], in0=hr[:], in1=bc[:, 1, :], op=ALU.add), reads=[hn, "bc"], writes=[hn])
            S.op("act", lambda: nc.scalar.copy(out=hbc[:], in_=hr[:]), reads=[hn], writes=["hbc"])
            for kg_ in range(4):
                pb, pbn = bbank()
                for kk in range(4):
                    k = kg_ * 4 + kk
                    S.op("pe", lambda: nc.tensor.transpose(out=pb[:, kk * 128:(kk + 1) * 128], in_=hbc[:, k * 128:(k + 1) * 128], identity=ident_b[:]),
                         reads=["hbc", "ident_b"], writes=[pbn], inc=(kk == 3))
                S.op("dve", lambda: nc.vector.tensor_copy(out=hTc[li][:, kg_ * 4:kg_ * 4 + 4, :], in_=pb[:, 0:512].rearrange("p (a b) -> p a b", a=4)),
                     reads=[pbn], writes=["hTc%d" % li])
        for cb in range(8):
            wt, wn = load_w(wg_v, KD, cb * 512, "wg16")
            for li, tt in enumerate(tiles):
                ps, pn = fbank()
                for k in range(KD):
                    S.op("pe", lambda: nc.tensor.matmul(ps[:], lhsT=hTc[li][:, k, :], rhs=wt[:, k, :], start=(k == 0), stop=False),
                         reads=["hTc%d" % li, wn], writes=[pn], inc=False)
                S.op("pe", lambda: nc.tensor.matmul(ps[:], lhsT=ones_b[0:1, :], rhs=bg_sb[0:1, cb * 512:(cb + 1) * 512], start=False, stop=True),
                     reads=["ones_b", "bg_sb"], writes=[pn])
                S.op("act", lambda: nc.scalar.activation(out=gate[li][:, cb * 512:(cb + 1) * 512], in_=ps[:], func=AF.Sigmoid), reads=[pn], writes=["gate%d" % li])
        for br in range(2):
            view = wpa_v if br == 0 else wpf_v
            for cb in range(4):
                wt, wn = load_w(view, 8, cb * 512, "wpa16" if br == 0 else "wpf16")
                for li, tt in enumerate(tiles):
                    yt_, ytn = yts[li]
                    ps, pn = fbank()
                    for k in range(8):
                        S.op("pe", lambda: nc.tensor.matmul(ps[:], lhsT=yt_[:, br * 8 + k, :], rhs=wt[:, k, :], start=(k == 0), stop=(k == 7)),
                             reads=[ytn, wn], writes=[pn], inc=(k == 7))
                    if br == 0:
                        S.op("dve", lambda: nc.vector.tensor_tensor(out=mrg[li][:, cb * 512:(cb + 1) * 512], in0=ps[:], in1=gate[li][:, cb * 512:(cb + 1) * 512], op=ALU.mult),
                             reads=[pn, "gate%d" % li], writes=["mrg%d" % li])
                    else:
                        S.op("dve", lambda: nc.vector.tensor_tensor(out=mtmp[:], in0=ps[:], in1=gate[li][:, 2048 + cb * 512:2048 + (cb + 1) * 512], op=ALU.mult),
                             reads=[pn, "gate%d" % li], writes=["mtmp"])
                        S.op("pool", lambda: nc.gpsimd.tensor_tensor(out=mb[li][:, cb * 512:(cb + 1) * 512], in0=mtmp[:], in1=mrg[li][:, cb * 512:(cb + 1) * 512], op=ALU.add),
                             reads=["mtmp", "mrg%d" % li], writes=["mb%d" % li])
        for li, tt in enumerate(tiles):
            for kg_ in range(4):
                pb, pbn = bbank()
                for kk in range(4):
                    k = kg_ * 4 + kk
                    S.op("pe", lambda: nc.tensor.transpose(out=pb[:, kk * 128:(kk + 1) * 128], in_=mb[li][:, k * 128:(k + 1) * 128], identity=ident_b[:]),
                         reads=["mb%d" % li, "ident_b"], writes=[pbn], inc=(kk == 3))
                S.op("act", lambda: nc.scalar.copy(out=mT[li][:, kg_ * 4:kg_ * 4 + 4, :], in_=pb[:, 0:512].rearrange("p (a b) -> p a b", a=4)), reads=[pbn], writes=["mT%d" % li])
        for cb in range(4):
            wt, wn = load_w(wo_v, KD, cb * 512, "wo16")
            for li, tt in enumerate(tiles):
                ps, pn = fbank()
                for k in range(KD):
                    S.op("pe", lambda: nc.tensor.matmul(ps[:], lhsT=mT[li][:, k, :], rhs=wt[:, k, :], start=(k == 0), stop=(k == KD - 1)),
                         reads=["mT%d" % li, wn], writes=[pn], inc=(k == KD - 1))
                S.op("dve", lambda: nc.vector.scalar_tensor_tensor(out=hres[li][:, cb * 512:(cb + 1) * 512], in0=hres[li][:, cb * 512:(cb + 1) * 512], scalar=ALPHA, in1=ps[:],
                                                                    op0=ALU.mult, op1=ALU.add),
                     reads=[pn, "hres%d" % li], writes=["hres%d" % li])
        for li, tt in enumerate(tiles):
            hr = hres[li]
            hn = "hres%d" % li
            ln_stats(hr, hn)
            S.op("act", lambda: nc.scalar.activation(out=hr[:], in_=hr[:], func=AF.Identity, bias=nmrc[:], scale=rstdc[:]), reads=[hn, "nmrc", "rstdc"], writes=[hn])
            S.op("dve", lambda: nc.vector.tensor_tensor(out=hr[:], in0=hr[:], in1=bc[:, 2, :], op=ALU.mult), reads=[hn, "bc"], writes=[hn])
            S.op("pool", lambda: nc.gpsimd.tensor_tensor(out=hr[:], in0=hr[:], in1=bc[:, 3, :], op=ALU.add), reads=[hn, "bc"], writes=[hn])
            S.dma(out[tt * 128:(tt + 1) * 128, :], hr[:], reads=[hn], writes=["out"], multi=True)
    S.finish("sp")
    esC.close()
    es_glob.close()
    return nc, dict(lay=lay, TG=TG, TC=TC, NTC=NTC, fac=fac, nins=S.nins)


def host_constants():
    k = np.arange(128)
    A0 = (k[:, None] <= k[None, :]).astype(np.float32)
    B0 = (k[:, None] > k[None, :]).astype(np.float32)
    maskA = np.zeros((128, 2, 144), np.float32)
    maskB = np.zeros((128, 2, 144), np.float32)
    maskA[:, 0, :128] = A0
    maskA[:, 1, :128] = A0.T
    maskB[:, 0, :128] = B0
    maskB[:, 1, :128] = B0.T
    maskA[:, :, 128] = 1.0
    maskB[:, :, 128] = 1.0
    negL = np.zeros((128, 2, 128), np.float32)
    negQ = np.zeros((128, 2, 128), np.float32)
    i = k[:, None]
    j = k[None, :]
    negL[:, 0, :] = np.where(i > j, 0.0, NEG)
    negL[:, 1, :] = np.where(i < j, 0.0, NEG)
    negQ[:, 0, :] = np.where(k[None, :] >= k[:, None], 0.0, NEG)
    negQ[:, 1, :] = np.where(k[None, :] <= k[:, None], 0.0, NEG)
    return dict(ident=np.eye(128, dtype=np.float32), maskA=maskA, maskB=maskB, negL=negL, negQ=negQ)


def fft_tables(T, n1, n2):
    a = np.arange(n1)
    th1 = 2 * np.pi * ((a[:, None] * a[None, :]) % n1) / n1
    c1 = np.stack([np.cos(th1), np.sin(th1), -np.sin(th1)], axis=1).astype(np.float32)
    nn2 = np.arange(n2, dtype=np.int64)
    k1 = np.arange(n1, dtype=np.int64)
    k2 = np.arange(n2, dtype=np.int64)
    kk = k1[:, None] + n1 * k2[None, :]
    ph = (nn2[:, None, None] * kk[None, :, :]) % T
    th = 2 * np.pi * ph.astype(np.float64) / T
    g = np.stack([np.cos(th), np.sin(th)], axis=1).astype(np.float32)
    return c1, g


def make_in_maps(seq_x, meta_tokens, ln_in_g, ln_in_b, w_in, b_gate, conv_w, a_log, dt_bias, dn_norm_w,
                 w_proj_a, w_proj_f, w_out, ln_g, ln_b, info):
    lay, TG, TC, fac = info["lay"], info["TG"], info["TC"], info["fac"]
    f32 = np.float32
    xall = np.zeros((TG, D_MODEL), f32)
    for (base, T, Tp), xs in zip(lay, seq_x):
        xall[base:base + N_META] = meta_tokens
        xall[base + N_META:base + T] = xs
    w = np.asarray(w_in[0], f32)
    cst = host_constants()
    bcast = np.zeros((128, 4, D_MODEL), f32)
    bcast[:, 0, :] = ln_in_g
    bcast[:, 1, :] = ln_in_b
    bcast[:, 2, :] = ln_g[0]
    bcast[:, 3, :] = ln_b[0]
    NTC = info["NTC"]
    shared = dict(xall=xall, ident=cst["ident"], maskA=cst["maskA"], maskB=cst["maskB"], negL=cst["negL"], negQ=cst["negQ"],
                  lng_pk=np.ascontiguousarray(np.asarray(ln_in_g, f32).reshape(KD, 128).T),
                  lnb_pk=np.ascontiguousarray(np.asarray(ln_in_b, f32).reshape(KD, 128).T),
                  dnw=np.ascontiguousarray(np.broadcast_to(np.asarray(dn_norm_w[0], f32)[None, :], (128, 128))),
                  wg=np.ascontiguousarray(w[:, COL_G:]), wpa=np.asarray(w_proj_a[0], f32), wpf=np.asarray(w_proj_f[0], f32),
                  wo=np.asarray(w_out[0], f32), bcast=bcast,
                  bgate=np.asarray(b_gate[0], f32).reshape(1, 4096))
    for T, (n1, n2) in fac.items():
        c1, g = fft_tables(T, n1, n2)
        shared["c1_%d" % T] = c1
        shared["g_%d" % T] = g
    maps = []
    cwf = np.asarray(conv_w[0], f32)
    for c in range(NCORE):
        h = c
        cols = np.zeros((D_MODEL, 1024), f32)
        cols[:, 0:128] = w[:, COL_Q + h * 128:COL_Q + (h + 1) * 128]
        cols[:, 128:256] = w[:, COL_K + h * 128:COL_K + (h + 1) * 128]
        cols[:, 256:384] = w[:, COL_V + h * 128:COL_V + (h + 1) * 128]
        cols[:, 384:512] = w[:, COL_ZA + h * 128:COL_ZA + (h + 1) * 128]
        grp = c // 2
        cols[:, 512:768] = w[:, COL_F + grp * 256:COL_F + (grp + 1) * 256]
        cols[:, 768:896] = w[:, COL_ZF + c * 128:COL_ZF + (c + 1) * 128]
        cols[:, 896] = w[:, COL_B + h]
        cols[:, 897] = w[:, COL_B + 8 + h]
        cols[:, 898] = w[:, COL_A + h]
        cols[:, 899] = w[:, COL_A + 8 + h]
        cwc = np.zeros((128, 15), f32)
        for a, cbase in enumerate((COL_Q, COL_K, COL_V)):
            cwc[:, a * 5:(a + 1) * 5] = cwf[:, cbase + h * 128:cbase + (h + 1) * 128].T
        ch = np.arange(256)
        cp = (c % 2) * 128 + np.arange(128)
        th = 2 * np.pi * ((ch[:, None] * cp[None, :]) % 256) / 256.0
        dchf = np.concatenate([np.cos(th), -np.sin(th)], axis=1).astype(f32)
        dchp = np.ascontiguousarray(dchf.reshape(2, 128, 256).transpose(1, 0, 2))
        gidx = np.zeros((128, NTC, 16), np.int32)
        p = np.arange(128)
        for tt in range(NTC):
            for r in range(8):
                gidx[:, tt, r] = ((r * NCORE + c) * 256 + p) * NTC + tt
                gidx[:, tt, 8 + r] = ((r * NCORE + c) * 256 + 128 + p) * NTC + tt
        m = dict(shared)
        m.update(xc=np.ascontiguousarray(xall[c * TC:(c + 1) * TC]), wc=cols, cw=cwc,
                 alog=np.ascontiguousarray(np.broadcast_to(np.asarray(a_log[0][:, h], f32)[None, :], (128, 2))),
                 dtb=np.ascontiguousarray(np.broadcast_to(np.asarray(dt_bias[0][:, h], f32)[None, :], (128, 2))),
                 dch=dchp, gidx=gidx)
        maps.append(m)
    return maps


def kernel(x_prompt, x_sample, meta_tokens, ln_in_g, ln_in_b, w_in, b_gate, conv_w, a_log, dt_bias,
           dn_norm_w, w_proj_a, w_proj_f, w_out, ln_g, ln_b):
    x_prompt = np.asarray(x_prompt, np.float32)
    x_sample = np.asarray(x_sample, np.float32)
    seq_x = [x_prompt[b] for b in range(x_prompt.shape[0])] + [x_sample[b] for b in range(x_sample.shape[0])]
    seq_lens = [N_META + s.shape[0] for s in seq_x]
    nc, info = build_program(seq_lens)
    maps = make_in_maps(seq_x, np.asarray(meta_tokens, np.float32), np.asarray(ln_in_g), np.asarray(ln_in_b), np.asarray(w_in),
                        np.asarray(b_gate), np.asarray(conv_w), np.asarray(a_log), np.asarray(dt_bias), np.asarray(dn_norm_w),
                        np.asarray(w_proj_a), np.asarray(w_proj_f), np.asarray(w_out), np.asarray(ln_g), np.asarray(ln_b), info)
    res = run_bass_kernel_spmd(nc, maps, core_ids=list(range(NCORE)))
    full = np.concatenate([res.results[c]["out"] for c in range(NCORE)], axis=0)
    outs = []
    for (base, T, Tp) in info["lay"]:
        outs.append(full[base + N_META:base + T])
    nb = x_prompt.shape[0]
    y_prompt = np.stack(outs[:nb], axis=0).astype(np.float32)
    y_sample = np.stack(outs[nb:], axis=0).astype(np.float32)
    return (y_prompt, y_sample)
```
